# Optimizing a Trainium2 kernel written in Bass

```python
import math
import jax
import jax.numpy as jnp
from jax import lax
import numpy as np

D_MODEL = 1024
BATCH = 4
SEQ = 8192
DEPTH = 4
DEC_BATCH = 32
DEC_SEQ = 32
PAST_LEN = 2048

CHUNK = 64
QBLOCK = 128
H_A = 4
DH_A = 64
DV_A = 2 * DH_A
W_A = H_A * DV_A
N_QK = H_A * 2 * DH_A
N_ATT = 2 * N_QK + W_A
H_B = 8
DH_B = 64
W_B = H_B * DH_B
W_LORA = 64
A_LORA = 64
G_LORA = 128
N_RWKV = 3 * W_B + W_LORA + A_LORA + G_LORA
N_IN = N_ATT + N_RWKV
D_FF = 2816
N_BUCKETS = 32
MAX_DISTANCE = 128
ALPHA = (2 * DEPTH) ** 0.25
BETA = (8 * DEPTH) ** -0.25
LN_EPS = 1e-5
GN_EPS = 64e-5

kernel_name = "streaming_diffattn_rwkv7_hybrid_step"


def layer_norm(x, g=None, b=None, eps=LN_EPS):
    xf = x.astype(jnp.float32)
    mu = jnp.mean(xf, axis=-1, keepdims=True)
    var = jnp.mean(jnp.square(xf - mu), axis=-1, keepdims=True)
    y = ((xf - mu) * lax.rsqrt(var + eps)).astype(x.dtype)
    if g is None:
        return y
    return y * g + b


def rms_norm(x, g, eps=LN_EPS):
    xf = x.astype(jnp.float32)
    y = (xf * lax.rsqrt(jnp.mean(jnp.square(xf), axis=-1, keepdims=True) + eps)).astype(x.dtype)
    return y * g


def rel_bucket(rel):
    nb = N_BUCKETS // 2
    ret = jnp.where(rel > 0, nb, 0)
    n = jnp.abs(rel)
    max_exact = nb // 2
    nf = jnp.maximum(n, 1).astype(jnp.float32)
    large = max_exact + (jnp.log(nf / max_exact) / math.log(MAX_DISTANCE / max_exact)
                         * (nb - max_exact)).astype(jnp.int32)
    large = jnp.minimum(large, nb - 1)
    return ret + jnp.where(n < max_exact, n, large)


def diff_attn_block(q, k, v, q_pos, k_pos, rel_bias, lam):
    s = jnp.einsum("bqhmd,bkhmd->bhmqk", q, k).astype(jnp.float32) * (DH_A ** -0.5)
    bias = jnp.take(rel_bias, rel_bucket(k_pos[None, :] - q_pos[:, None]), axis=0)
    s = s + jnp.transpose(bias, (2, 0, 1)).astype(jnp.float32)[None, :, None]
    visible = (k_pos[None, :] // CHUNK) <= (q_pos[:, None] // CHUNK)
    s = jnp.where(visible, s, -jnp.inf)
    p = jax.nn.softmax(s, axis=-1)
    a = (p[:, :, 0] - lam * p[:, :, 1]).astype(v.dtype)
    return jnp.einsum("bhqk,bkhd->bqhd", a, v)


def diff_attention(q, k, v, q_pos, k_pos, rel_bias, lam):
    b, sq = q.shape[:2]
    qb = QBLOCK if sq % QBLOCK == 0 else sq
    nblk = sq // qb
    q_blocks = jnp.swapaxes(q.reshape(b, nblk, qb, H_A, 2, DH_A), 0, 1)
    pos_blocks = q_pos.reshape(nblk, qb)
    out = lax.map(lambda qp: diff_attn_block(qp[0], k, v, qp[1], k_pos, rel_bias, lam),
                  (q_blocks, pos_blocks))
    return jnp.swapaxes(out, 0, 1).reshape(b, sq, H_A, DV_A)


def diff_mixer(p_att, past_k, past_v, q_pos, k_pos, lam_qk, subln_g, rel_bias, layer_idx):
    b, s = p_att.shape[:2]
    q = p_att[..., :N_QK].reshape(b, s, H_A, 2, DH_A)
    k_new = p_att[..., N_QK:2 * N_QK].reshape(b, s, H_A, 2 * DH_A)
    v_new = p_att[..., 2 * N_QK:].reshape(b, s, H_A, DV_A)
    if past_k is None:
        k_all, v_all = k_new, v_new
    else:
        k_all = jnp.concatenate([past_k.astype(k_new.dtype), k_new], axis=1)
        v_all = jnp.concatenate([past_v.astype(v_new.dtype), v_new], axis=1)
    lam_init = 0.8 - 0.6 * math.exp(-0.3 * layer_idx)
    lq = lam_qk.astype(jnp.float32)
    lam = jnp.exp(jnp.sum(lq[0] * lq[1])) - jnp.exp(jnp.sum(lq[2] * lq[3])) + lam_init
    o = diff_attention(q, k_all.reshape(b, -1, H_A, 2, DH_A), v_all, q_pos, k_pos, rel_bias, lam)
    o = rms_norm(o, subln_g) * (1.0 - lam_init)
    return o.reshape(b, s, W_A), k_new, v_new


def wkv_step(state, inp):
    r_t, w_t, k_t, v_t, kk_t, a_t = inp
    sa = jnp.einsum("bhvk,bhk->bhv", state, -kk_t)
    state = (state * w_t[:, :, None, :] + sa[..., None] * (kk_t * a_t)[:, :, None, :]
             + v_t[..., None] * k_t[:, :, None, :])
    return state, jnp.einsum("bhvk,bhk->bhv", state, r_t)


def rwkv_mixer(p_rw, shift, wkv, mu, w0, w2, a0, a2, g2, k_k, k_a, r_k, lnx_g, lnx_b):
    b, s = p_rw.shape[:2]
    prev = jnp.concatenate([shift[:, None].astype(p_rw.dtype), p_rw[:, :-1]], axis=1)
    xm = p_rw + (prev - p_rw) * mu
    o0 = 3 * W_B
    r = xm[..., :W_B]
    k = xm[..., W_B:2 * W_B]
    v = xm[..., 2 * W_B:o0]
    dw = xm[..., o0:o0 + W_LORA]
    da = xm[..., o0 + W_LORA:o0 + W_LORA + A_LORA]
    dg = xm[..., o0 + W_LORA + A_LORA:]
    w_log = -jax.nn.softplus(-(w0 + jnp.tanh(dw) @ w2)) - 0.5
    decay = jnp.exp(-jnp.exp(w_log.astype(jnp.float32)))
    a = jax.nn.sigmoid(a0 + da @ a2)
    g = jax.nn.sigmoid(dg) @ g2

    def heads(t):
        return t.reshape(b, s, H_B, DH_B)

    kk = heads(k * k_k).astype(jnp.float32)
    kk = kk * lax.rsqrt(jnp.maximum(jnp.sum(kk * kk, axis=-1, keepdims=True), 1e-24))
    k = k * (1.0 + (a - 1.0) * k_a)
    rh, kh, vh, ah = heads(r), heads(k), heads(v), heads(a)

    def tmajor(t):
        return jnp.swapaxes(t.astype(jnp.float32), 0, 1)

    xs = (tmajor(rh), tmajor(heads(decay)), tmajor(kh), tmajor(vh), tmajor(kk), tmajor(ah))
    s_final, o = lax.scan(wkv_step, wkv.astype(jnp.float32), xs)
    o = jnp.swapaxes(o, 0, 1)
    m = jnp.mean(o, axis=-1, keepdims=True)
    var = jnp.mean(jnp.square(o - m), axis=-1, keepdims=True)
    o = ((o - m) * lax.rsqrt(var + GN_EPS)).astype(p_rw.dtype).reshape(b, s, W_B) * lnx_g + lnx_b
    bonus = jnp.sum(rh * kh * r_k, axis=-1, keepdims=True) * vh
    o = (o + bonus.reshape(b, s, W_B)) * g
    return o, p_rw[:, -1], s_final.astype(p_rw.dtype)


def swiglu(u, w_gu, w_down):
    h = u @ w_gu
    return (jax.nn.silu(h[..., :D_FF]) * h[..., D_FF:]) @ w_down


def trunk_layer(x, c, l, W, rel_bias, past_k, past_v, wkv, shift, q_pos, k_pos):
    b = x.shape[0]
    mod = (jax.nn.silu(c) @ W["w_ada"][l] + W["b_ada"][l]).reshape(b, 3, 3, D_MODEL)

    def adaln(h, i):
        return layer_norm(h) * (1.0 + mod[:, i, 1, None]) + mod[:, i, 0, None]

    f = swiglu(adaln(x, 0), W["w_gu"][l, 0], W["w_down"][l, 0])
    x = layer_norm(ALPHA * x + 0.5 * mod[:, 0, 2, None] * f, W["ln_g"][l, 0], W["ln_b"][l, 0])
    u = adaln(x, 1)
    proj = u @ W["w_in"][l]
    y_a, k_new, v_new = diff_mixer(proj[..., :N_ATT], past_k, past_v, q_pos, k_pos,
                                   W["lam_qk"][l], W["subln_g"][l], rel_bias, l)
    y_b, shift_new, wkv_new = rwkv_mixer(proj[..., N_ATT:], shift, wkv, W["rw_mu"][l], W["rw_w0"][l],
                                         W["rw_w2"][l], W["rw_a0"][l], W["rw_a2"][l], W["rw_g2"][l],
                                         W["rw_k_k"][l], W["rw_k_a"][l], W["rw_r_k"][l],
                                         W["rw_lnx_g"][l], W["rw_lnx_b"][l])
    gates = jax.nn.sigmoid(u @ W["w_gate"][l] + W["b_gate"][l])
    merged = (gates[..., :D_MODEL] * (y_a @ W["w_br_a"][l])
              + gates[..., D_MODEL:] * (y_b @ W["w_br_b"][l]))
    x = layer_norm(ALPHA * x + mod[:, 1, 2, None] * (merged @ W["w_o"][l]),
                   W["ln_g"][l, 1], W["ln_b"][l, 1])
    f = swiglu(adaln(x, 2), W["w_gu"][l, 1], W["w_down"][l, 1])
    x = layer_norm(ALPHA * x + 0.5 * mod[:, 2, 2, None] * f, W["ln_g"][l, 2], W["ln_b"][l, 2])
    return x, k_new, v_new, wkv_new, shift_new


def setup_inputs(seed: int = 0) -> dict:
    key = jax.random.key(seed)
    keys = iter(jax.random.split(key, 40))

    def nrm(shape, scale):
        return scale * jax.random.normal(next(keys), shape, jnp.float32)

    def uni(shape, lo, hi):
        return jax.random.uniform(next(keys), shape, jnp.float32, lo, hi)

    L, D = DEPTH, D_MODEL
    return {
        "x_prompt": nrm((BATCH, SEQ, D), 1.0),
        "x_sample": nrm((DEC_BATCH, DEC_SEQ, D), 1.0),
        "c_prompt": nrm((BATCH, D), 1.0),
        "c_sample": nrm((DEC_BATCH, D), 1.0),
        "cache_k": nrm((L, DEC_BATCH, PAST_LEN, H_A, 2 * DH_A), 1.0),
        "cache_v": nrm((L, DEC_BATCH, PAST_LEN, H_A, DV_A), 1.0),
        "state_wkv": nrm((L, DEC_BATCH, H_B, DH_B, DH_B), 0.5),
        "state_shift": nrm((L, DEC_BATCH, N_RWKV), 1.0),
        "rel_bias": nrm((N_BUCKETS, H_A), 0.5),
        "w_ada": nrm((L, D, 9 * D), 0.5 * D ** -0.5),
        "b_ada": nrm((L, 9 * D), 0.01),
        "ln_g": 1.0 + nrm((L, 3, D), 0.05),
        "ln_b": nrm((L, 3, D), 0.01),
        "w_gu": nrm((L, 2, D, 2 * D_FF), D ** -0.5),
        "w_down": nrm((L, 2, D_FF, D), BETA * D_FF ** -0.5),
        "w_in": nrm((L, D, N_IN), D ** -0.5),
        "lam_qk": nrm((L, 4, DH_A), 0.1),
        "subln_g": 1.0 + nrm((L, DV_A), 0.1),
        "rw_mu": uni((L, N_RWKV), 0.0, 1.0),
        "rw_w0": uni((L, W_B), -4.0, 1.0),
        "rw_w2": nrm((L, W_LORA, W_B), 0.1 * W_LORA ** -0.5),
        "rw_a0": nrm((L, W_B), 0.1),
        "rw_a2": nrm((L, A_LORA, W_B), 0.1 * A_LORA ** -0.5),
        "rw_g2": nrm((L, G_LORA, W_B), G_LORA ** -0.5),
        "rw_k_k": 0.85 + nrm((L, W_B), 0.05),
        "rw_k_a": 1.0 + nrm((L, W_B), 0.05),
        "rw_r_k": nrm((L, H_B, DH_B), 0.1),
        "rw_lnx_g": 1.0 + nrm((L, W_B), 0.1),
        "rw_lnx_b": nrm((L, W_B), 0.01),
        "w_br_a": nrm((L, W_A, D), BETA * W_A ** -0.5),
        "w_br_b": nrm((L, W_B, D), BETA * W_B ** -0.5),
        "w_gate": nrm((L, D, 2 * D), D ** -0.5),
        "b_gate": nrm((L, 2 * D), 0.01),
        "w_o": nrm((L, D, D), BETA * D ** -0.5),
    }


def reference(x_prompt, x_sample, c_prompt, c_sample, cache_k, cache_v, state_wkv, state_shift,
              rel_bias, w_ada, b_ada, ln_g, ln_b, w_gu, w_down, w_in, lam_qk, subln_g,
              rw_mu, rw_w0, rw_w2, rw_a0, rw_a2, rw_g2, rw_k_k, rw_k_a, rw_r_k, rw_lnx_g, rw_lnx_b,
              w_br_a, w_br_b, w_gate, b_gate, w_o):
    W = {"w_ada": w_ada, "b_ada": b_ada, "ln_g": ln_g, "ln_b": ln_b, "w_gu": w_gu,
         "w_down": w_down, "w_in": w_in, "lam_qk": lam_qk, "subln_g": subln_g,
         "rw_mu": rw_mu, "rw_w0": rw_w0, "rw_w2": rw_w2, "rw_a0": rw_a0, "rw_a2": rw_a2,
         "rw_g2": rw_g2, "rw_k_k": rw_k_k, "rw_k_a": rw_k_a, "rw_r_k": rw_r_k,
         "rw_lnx_g": rw_lnx_g, "rw_lnx_b": rw_lnx_b, "w_br_a": w_br_a, "w_br_b": w_br_b,
         "w_gate": w_gate, "b_gate": b_gate, "w_o": w_o}
    b_p, s_p = x_prompt.shape[:2]
    past = cache_k.shape[2]
    s_d = x_sample.shape[1]
    pos_p = jnp.arange(s_p, dtype=jnp.int32)
    pos_d = past + jnp.arange(s_d, dtype=jnp.int32)
    k_pos_d = jnp.concatenate([jnp.arange(past, dtype=jnp.int32), pos_d])
    zero_wkv = jnp.zeros((b_p, H_B, DH_B, DH_B), x_prompt.dtype)
    zero_shift = jnp.zeros((b_p, N_RWKV), x_prompt.dtype)

    hp, hd = x_prompt, x_sample
    kps, vps, sps, shps = [], [], [], []
    kds, vds, sds, shds = [], [], [], []
    for l in range(DEPTH):
        hp, kp, vp, sp, shp = trunk_layer(hp, c_prompt, l, W, rel_bias, None, None,
                                          zero_wkv, zero_shift, pos_p, pos_p)
        hd, kd, vd, sd, shd = trunk_layer(hd, c_sample, l, W, rel_bias, cache_k[l], cache_v[l],
                                          state_wkv[l], state_shift[l], pos_d, k_pos_d)
        kps.append(kp); vps.append(vp); sps.append(sp); shps.append(shp)
        kds.append(kd); vds.append(vd); sds.append(sd); shds.append(shd)
    return (hp, hd, jnp.stack(kps), jnp.stack(vps), jnp.stack(sps), jnp.stack(shps),
            jnp.stack(kds), jnp.stack(vds), jnp.stack(sds), jnp.stack(shds))
```

```python
import contextlib
import math
import numpy as np
import concourse.bass as bass
import concourse.mybir as mybir
from concourse.bass_utils import run_bass_kernel_spmd

F32 = mybir.dt.float32
BF16 = mybir.dt.bfloat16
AF = mybir.ActivationFunctionType
ALU = mybir.AluOpType
AX = mybir.AxisListType

N_DMA_SEMS = 12

D = 1024
H_A = 4
N_QK = 512
W_A = 512
N_ATT = 1536
W_B = 512
N_RWKV = 1792
N_IN = 3328
D_FF = 2816
NJ = 22
LN_EPS = 1e-5
GN_EPS = 64e-5
NSMP = 4
DEC_SEQ = 32
TT = 256


class Op:
    __slots__ = ("eng", "emit", "deps", "sig", "sig_idx", "is_dma", "dsem", "dval", "prev_dma")

    def __init__(self, eng, emit, is_dma=False):
        self.eng = eng
        self.emit = emit
        self.deps = []
        self.sig = False
        self.sig_idx = 0
        self.is_dma = is_dma
        self.dsem = None
        self.dval = 0
        self.prev_dma = None


class Sched:
    ENGS = ("pe", "act", "dve", "pool", "sp")

    def __init__(self, nc):
        self.nc = nc
        self.ops = {e: [] for e in self.ENGS}
        self.bufs = {}
        self.dma_cnt = {e: 0 for e in self.ENGS}
        self.dma_last = {}
        self.stack = contextlib.ExitStack()

    def sbuf(self, name, shape, dtype=F32):
        return self.stack.enter_context(self.nc.sbuf_tensor(name, list(shape), dtype))

    def psum(self, name, shape, dtype=F32):
        return self.stack.enter_context(self.nc.psum_tensor(name, list(shape), dtype))

    def _track(self, op, reads, writes):
        deps = op.deps
        for k in reads:
            b = self.bufs.setdefault(k, [None, []])
            if b[0] is not None:
                deps.append(b[0])
            b[1].append(op)
        for k in writes:
            b = self.bufs.setdefault(k, [None, []])
            if b[0] is not None:
                deps.append(b[0])
            for r in b[1]:
                if r is not op:
                    deps.append(r)
            b[0] = op
            b[1] = []

    def op(self, eng, emit, reads=(), writes=()):
        o = Op(eng, emit)
        self._track(o, reads, writes)
        self.ops[eng].append(o)
        return o

    def dma(self, eng, out, in_, reads=(), writes=(), **kw):
        o = Op(eng, lambda e: e.dma_start(out=out, in_=in_, **kw), is_dma=True)
        self._track(o, reads, writes)
        j = self.dma_cnt[eng] % N_DMA_SEMS
        self.dma_cnt[eng] += 1
        key = (eng, j)
        o.dsem = key
        prev = self.dma_last.get(key)
        o.prev_dma = prev
        o.dval = (prev.dval if prev is not None else 0) + 16
        self.dma_last[key] = o
        self.ops[eng].append(o)
        return o

    def finish(self):
        nc = self.nc
        for e in self.ENGS:
            for o in self.ops[e]:
                for d in o.deps:
                    if d.is_dma:
                        continue
                    if d.eng == "pe" and o.eng == "pe" and not o.is_dma:
                        continue
                    d.sig = True
        for e in self.ENGS:
            n = 0
            for o in self.ops[e]:
                if o.sig and not o.is_dma:
                    n += 1
                    o.sig_idx = n
        st = self.stack
        esem = {e: st.enter_context(nc.semaphore("s_" + e)) for e in self.ENGS}
        dsem = {}
        for e in self.ENGS:
            for j in range(min(N_DMA_SEMS, self.dma_cnt[e])):
                dsem[(e, j)] = st.enter_context(nc.semaphore("d_%s_%d" % (e, j)))
        block = st.enter_context(nc.Block())

        def run(e, h):
            waited = {}
            for o in self.ops[e]:
                need = {}
                for d in o.deps:
                    if d.is_dma:
                        s, v = dsem[d.dsem], d.dval
                    else:
                        if d.eng == "pe" and e == "pe" and not o.is_dma:
                            continue
                        s, v = esem[d.eng], d.sig_idx
                    if v > need.get(s, 0):
                        need[s] = v
                if o.is_dma and o.prev_dma is not None:
                    s, v = dsem[o.dsem], o.prev_dma.dval
                    if v > need.get(s, 0):
                        need[s] = v
                for s, v in need.items():
                    if waited.get(s, 0) < v:
                        h.wait_ge(s, v)
                        waited[s] = v
                ins = o.emit(h)
                if o.is_dma:
                    ins.then_inc(dsem[o.dsem], 16)
                elif o.sig:
                    ins.then_inc(esem[e], 1)
            if e == "sp":
                for key, last in self.dma_last.items():
                    h.wait_ge(dsem[key], last.dval)

        @block.tensor
        def _(h):
            run("pe", h)

        @block.scalar
        def _(h):
            run("act", h)

        @block.vector
        def _(h):
            run("dve", h)

        @block.gpsimd
        def _(h):
            run("pool", h)

        @block.sync
        def _(h):
            run("sp", h)

    def close(self):
        self.stack.close()


def _rel_bucket(rel):
    nb = 16
    ret = np.where(rel > 0, nb, 0)
    n = np.abs(rel)
    max_exact = 8
    nf = np.maximum(n, 1).astype(np.float32)
    large = max_exact + (np.log(nf / np.float32(max_exact)) / np.float32(math.log(128 / max_exact))
                         * np.float32(nb - max_exact)).astype(np.int32)
    large = np.minimum(large, nb - 1)
    return ret + np.where(n < max_exact, n, large)


def _consts():
    j = np.arange(128)[:, None]
    i = np.arange(128)[None, :]
    c = {}
    c["c_ident"] = np.eye(128, dtype=np.float32)
    c["c_idx0"] = _rel_bucket(j - i).astype(np.float32)
    c["c_idxm1"] = _rel_bucket(j - i - 128).astype(np.float32)
    c["c_vis0"] = np.where((j >= 64) & (i < 64), 0.0, 1.0).astype(np.float32)
    s = np.arange(64)[:, None]
    t = np.arange(64)[None, :]
    strict = (s < t).astype(np.float32)
    incl = (s <= t).astype(np.float32)
    c["c_mask_g"] = np.concatenate([-strict, incl], axis=1).astype(np.float32)
    c["c_mask_gk"] = np.concatenate([strict, incl], axis=1).astype(np.float32)
    c["c_mask_a"] = (-(strict.T)).astype(np.float32)
    c["c_mask_g4"] = np.ascontiguousarray(np.tile(c["c_mask_g"][:, None, :], (2, 4, 1)))
    c["c_mask_gk4"] = np.ascontiguousarray(np.tile(c["c_mask_gk"][:, None, :], (2, 4, 1)))
    c["c_mask_a8"] = np.ascontiguousarray(np.tile(c["c_mask_a"][:, None, :], (2, 4, 1)))
    del c["c_mask_g"], c["c_mask_gk"], c["c_mask_a"]
    hs = np.zeros((128, 128), np.float32)
    hs[:64, :64] = 1.0
    hs[64:, 64:] = 1.0
    c["c_headsum"] = hs
    return c


ARENA_F32 = 50 * 1024


def build(SEQ, L, PAST, dbg=None, depth=None):
    NTP = SEQ // TT
    NT = NTP + 1
    ALPHA = (2 * (depth or L)) ** 0.25
    NPT = PAST // 128
    nc = bass.Bass("TRN2", target_bir_lowering=False)

    def din(name, shape):
        return nc.dram_tensor(name, list(shape), F32, kind="ExternalInput").ap()

    def dout(name, shape):
        return nc.dram_tensor(name, list(shape), F32, kind="ExternalOutput").ap()

    def dscr(name, shape, dt=F32):
        if dbg and name in dbg:
            return nc.dram_tensor(name, list(shape), dt, kind="ExternalOutput").ap()
        return nc.dram_tensor(name, list(shape), dt).ap()

    I = {}
    I["xp"] = din("xp", [SEQ, D]); I["xs"] = din("xs", [128, D]); I["cc"] = din("cc", [5, D])
    I["ck"] = din("ck", [L, NSMP, PAST, 512]); I["cv"] = din("cv", [L, NSMP, PAST, 512])
    I["swkv"] = din("swkv", [L, NSMP, 8, 64, 64]); I["sshift"] = din("sshift", [L, NSMP, N_RWKV])
    I["rel_bias"] = din("rel_bias", [1, 128])
    I["w_ada"] = din("w_ada", [L, D, 9 * D]); I["vecA"] = din("vecA", [L, 120, 128])
    I["vecB"] = din("vecB", [L, 58, 128])
    I["w_gu"] = din("w_gu", [L, 2, D, 2 * D_FF]); I["w_down"] = din("w_down", [L, 2, D_FF, D])
    I["w_in"] = din("w_in", [L, D, N_IN]); I["lam_qk"] = din("lam_qk", [L, 256])
    I["subln_g"] = din("subln_g", [L, 128]); I["lamv"] = din("lamv", [L, 2])
    I["rw_w2"] = din("rw_w2", [L, 64, 512]); I["rw_a2"] = din("rw_a2", [L, 64, 512])
    I["rw_g2"] = din("rw_g2", [L, 128, 512])
    I["w_br_a"] = din("w_br_a", [L, 512, D]); I["w_br_b"] = din("w_br_b", [L, 512, D])
    I["w_gate"] = din("w_gate", [L, D, 2 * D]); I["w_o"] = din("w_o", [L, D, D])
    for k, v in _consts().items():
        I[k] = din(k, list(v.shape))
    O = {}
    O["y_p"] = dout("y_p", [SEQ, D]); O["y_s"] = dout("y_s", [128, D])
    O["nk_p"] = dout("nk_p", [L, SEQ, 512]); O["nv_p"] = dout("nv_p", [L, SEQ, 512])
    O["nwkv_p"] = dout("nwkv_p", [L, 8, 64, 64]); O["nsh_p"] = dout("nsh_p", [L, N_RWKV])
    O["nk_s"] = dout("nk_s", [L, 128, 512]); O["nv_s"] = dout("nv_s", [L, 128, 512])
    O["nwkv_s"] = dout("nwkv_s", [L, NSMP, 8, 64, 64]); O["nsh_s"] = dout("nsh_s", [L, NSMP, N_RWKV])
    xT_s = dscr("xT_s", [NT, 128, 8, TT])
    uT_s = dscr("uT_s", [NT, 128, 8, TT], BF16)
    qT_s = dscr("qT_s", [NT, 128, 4, TT], BF16)
    kT_s = dscr("kT_s", [4, 128, SEQ], BF16)
    v_s = dscr("v_s", [4, 128, SEQ // 128, 128], BF16)
    kTs_s = dscr("kTs_s", [NSMP, 4, 128, DEC_SEQ], BF16)
    vs_s = dscr("vs_s", [NSMP, 4, DEC_SEQ, 128], BF16)
    prw_s = dscr("prw_s", [NT, 128, 14, TT])
    yaT_s = dscr("yaT_s", [NT, 128, 4, TT], BF16)
    ybT_s = dscr("ybT_s", [NT, 128, 4, TT], BF16)

    S = Sched(nc)
    arena = S.sbuf("arena", [128, ARENA_F32], F32)
    PS = [S.psum("ps%d" % i, [128, 512], F32) for i in range(8)]
    st = {"off": 0, "uid": 0, "psr": 0}

    def a_reset(keep=0):
        st["off"] = keep

    def a_f32(shape):
        n = int(np.prod(shape[1:]))
        v = arena[0:shape[0], st["off"]:st["off"] + n]
        st["off"] += n
        assert st["off"] <= ARENA_F32, st["off"]
        if len(shape) == 3:
            v = v.rearrange("p (a b) -> p a b", b=shape[2])
        elif len(shape) == 4:
            v = v.rearrange("p (a b c) -> p a b c", b=shape[2], c=shape[3])
        return v

    def a_bf16(shape):
        n = int(np.prod(shape[1:]))
        nf = (n + 1) // 2
        v = arena[0:shape[0], st["off"]:st["off"] + nf].bitcast(BF16)
        st["off"] += nf
        assert st["off"] <= ARENA_F32, st["off"]
        if n != 2 * nf:
            v = v[:, 0:n]
        if len(shape) == 3:
            v = v.rearrange("p (a b) -> p a b", b=shape[2])
        elif len(shape) == 4:
            v = v.rearrange("p (a b c) -> p a b c", b=shape[2], c=shape[3])
        return v

    def uid(s):
        st["uid"] += 1
        return "%s#%d" % (s, st["uid"])

    def barrier():
        lasts = []
        for e in S.ENGS:
            if S.ops[e]:
                lasts.append(S.ops[e][-1])
        lasts += list(S.dma_last.values())
        S.pending = {e: list(lasts) for e in S.ENGS}

    S.pending = {}
    _op0, _dma0 = S.op, S.dma

    def op(eng, emit, reads=(), writes=()):
        o = _op0(eng, emit, reads, writes)
        p = S.pending.pop(eng, None)
        if p:
            o.deps.extend(p)
        return o

    def dma(eng, out, in_, reads=(), writes=(), **kw):
        o = _dma0(eng, out, in_, reads, writes, **kw)
        p = S.pending.pop(eng, None)
        if p:
            o.deps.extend(p)
        return o

    def mm(out, lhsT, rhs, start, stop, reads, writes):
        return op("pe", lambda e: e.matmul(out, lhsT, rhs, start=start, stop=stop), reads, writes)

    def tpose(out, in_, ident, reads, writes):
        return op("pe", lambda e: e.transpose(out, in_, ident), reads, writes)

    def tt(eng, out, a, b, alu, reads, writes):
        return op(eng, lambda e: e.tensor_tensor(out=out, in0=a, in1=b, op=alu), reads, writes)

    def ts(eng, out, a, s1, s2, op0, op1, reads, writes):
        if s2 is None:
            return op(eng, lambda e: e.tensor_scalar(out=out, in0=a, scalar1=s1, scalar2=None, op0=op0), reads, writes)
        return op(eng, lambda e: e.tensor_scalar(out=out, in0=a, scalar1=s1, scalar2=s2, op0=op0, op1=op1), reads, writes)

    def stt(eng, out, in0, scalar, in1, op0, op1, reads, writes):
        return op("dve", lambda e: e.scalar_tensor_tensor(out=out, in0=in0, scalar=scalar, in1=in1, op0=op0, op1=op1),
                  reads, writes)

    def act(out, in_, func, reads, writes, bias=None, scale=1.0):
        if bias is None:
            return op("act", lambda e: e.activation(out=out, in_=in_, func=func, scale=scale), reads, writes)
        return op("act", lambda e: e.activation(out=out, in_=in_, func=func, bias=bias, scale=scale), reads, writes)

    def cp(eng, out, in_, reads, writes):
        if eng == "act":
            return op("act", lambda e: e.copy(out=out, in_=in_), reads, writes)
        return op(eng, lambda e: e.tensor_copy(out=out, in_=in_), reads, writes)

    def memset(eng, ap, val, writes):
        return op(eng, lambda e: e.memset(ap, val), (), writes)

    def ps_next():
        i = st["psr"] % 8
        st["psr"] += 1
        return PS[i], "ps%d" % i

    rr = {"ev": 0, "cast": 0}

    def ev_eng():
        rr["ev"] += 1
        return ("dve", "act")[rr["ev"] % 2]

    def cast_eng():
        rr["cast"] += 1
        return ("dve", "pool", "act")[rr["cast"] % 3]

    ident = a_f32([128, 128]); onesb = a_bf16([128, 128]); hsum = a_f32([128, 128])
    epsc = a_f32([128, 4])
    silc = a_f32([128, 8, 5])
    modT = a_f32([128, 72, 5])
    vA = a_f32([128, 120]); vB = a_f32([128, 58])
    EB = a_bf16([128, 8, 128])
    lamc = a_f32([128, 4])
    subg = a_f32([128, 1])
    maskg = a_f32([128, 4, 128]); maskgk = a_f32([128, 4, 128]); maska = a_f32([128, 4, 64])
    Hst = a_f32([128, 5, 4, 64])
    shT = a_f32([128, 14, 5])
    KEEP = st["off"]

    dma("sp", ident, I["c_ident"], writes=["ident"])
    dma("sp", hsum, I["c_headsum"], writes=["hsum"])
    dma("sp", maskg, I["c_mask_g4"], writes=["maskg"])
    dma("sp", maskgk, I["c_mask_gk4"], writes=["maskgk"])
    dma("sp", maska, I["c_mask_a8"], writes=["maska"])
    memset("pool", onesb, 1.0, ["onesb"])
    memset("pool", epsc[:, 0:1], LN_EPS, ["epsc"])
    memset("pool", epsc[:, 1:2], GN_EPS, ["epsc"])
    memset("pool", epsc[:, 2:3], 0.0, ["epsc"])

    def tile_info(ti):
        if ti < NTP:
            return TT, [(0, TT, 0)]
        return 128, [(s * 32, 32, 1 + s) for s in range(NSMP)]

    def phase0():
        a_reset(KEEP)
        rbc = a_f32([128, 128]); idx = a_f32([128, 2, 128]); vis = a_f32([128, 128])
        acc = a_f32([128, 128]); tmp = a_f32([128, 128])
        dma("sp", rbc, I["rel_bias"].partition_broadcast(128), writes=["rbc"])
        dma("sp", idx[:, 0, :], I["c_idx0"], writes=["idx"])
        dma("sp", idx[:, 1, :], I["c_idxm1"], writes=["idx"])
        dma("sp", vis, I["c_vis0"], writes=["vis"])
        for h in range(4):
            for w in range(2):
                memset("dve", acc, 0.0, ["acc"])
                for b in range(32):
                    ts("dve", tmp, idx[:, w, :], float(b), rbc[:, b * 4 + h:b * 4 + h + 1], ALU.is_equal, ALU.mult,
                       ["idx", "rbc"], ["tmp"])
                    tt("dve", acc, acc, tmp, ALU.add, ["acc", "tmp"], ["acc"])
                ts("dve", acc, acc, rbc[:, 60 + h:61 + h], None, ALU.subtract, None, ["acc", "rbc"], ["acc"])
                act(acc, acc, AF.Exp, ["acc"], ["acc"])
                if w == 0:
                    tt("dve", EB[:, h * 2 + w, :], acc, vis, ALU.mult, ["acc", "vis"], ["EB"])
                else:
                    cp("dve", EB[:, h * 2 + w, :], acc, ["acc"], ["EB"])
        cT = a_f32([128, 8, 5]); sg = a_f32([128, 8, 5])
        for s in range(5):
            dma("sp", cT[:, :, s], I["cc"][s].rearrange("(k p) -> p k", p=128), writes=["cT"],
                allow_slow_non_contiguous=True)
        act(sg, cT, AF.Sigmoid, ["cT"], ["sg"])
        tt("dve", silc, cT, sg, ALU.mult, ["cT", "sg"], ["silc"])
        xin = [a_f32([128, D]) for _ in range(2)]
        xo = [a_f32([128, 8, 128]) for _ in range(2)]
        for ti in range(NT):
            N, _ = tile_info(ti)
            for sb in range(N // 128):
                r = (ti * (TT // 128) + sb) % 2
                src = I["xp"][ti * TT + sb * 128: ti * TT + (sb + 1) * 128, :] if ti < NTP else I["xs"]
                dma("sp", xin[r], src, writes=["xin%d" % r])
                for k in range(8):
                    ps, pk = ps_next()
                    tpose(ps[:, 0:128], xin[r][:, k * 128:(k + 1) * 128], ident, ["xin%d" % r, "ident"], [pk])
                    cp(ev_eng(), xo[r][:, k, :], ps[:, 0:128], [pk], [("xo", r, k)])
                dma("pool", xT_s[ti, :, :, sb * 128:(sb + 1) * 128], xo[r],
                    reads=[("xo", r, k) for k in range(8)], writes=[("xT", ti)])

    phase0()
    barrier()

    def layer_setup(l):
        a_reset(KEEP)
        sA = a_f32([120, 128]); sB = a_f32([58, 128])
        dma("sp", sA, I["vecA"][l], writes=["sA"])
        dma("sp", sB, I["vecB"][l], writes=["sB"])
        ps, pk = ps_next()
        tpose(ps[:, 0:120], sA, ident[0:120, 0:120], ["sA", "ident"], [pk])
        cp("dve", vA, ps[:, 0:120], [pk], ["vA"])
        ps, pk = ps_next()
        tpose(ps[:, 0:58], sB, ident[0:58, 0:58], ["sB", "ident"], [pk])
        cp("dve", vB, ps[:, 0:58], [pk], ["vB"])
        lq = a_f32([128, 256]); pr = a_f32([128, 128]); sm = a_f32([128, 2]); lv = a_f32([128, 2])
        dma("sp", lv, I["lamv"][l:l + 1, :].partition_broadcast(128), writes=["lv"])
        dma("sp", lq, I["lam_qk"][l:l + 1, :].partition_broadcast(128), writes=["lq"])
        tt("dve", pr[:, 0:64], lq[:, 0:64], lq[:, 64:128], ALU.mult, ["lq"], ["pr"])
        tt("dve", pr[:, 64:128], lq[:, 128:192], lq[:, 192:256], ALU.mult, ["lq"], ["pr"])
        op("dve", lambda e: e.reduce_sum(out=sm[:, 0:1], in_=pr[:, 0:64], axis=AX.X), ["pr"], ["sm"])
        op("dve", lambda e: e.reduce_sum(out=sm[:, 1:2], in_=pr[:, 64:128], axis=AX.X), ["pr"], ["sm"])
        act(sm, sm, AF.Exp, ["sm"], ["sm"])
        stt("dve", lamc[:, 0:1], sm[:, 1:2], lv[:, 0:1], sm[:, 0:1], ALU.subtract, ALU.subtract, ["sm", "lv"], ["lamc"])
        cp("dve", lamc[:, 1:2], lv[:, 1:2], ["lv"], ["lamc"])
        dma("sp", subg, I["subln_g"][l].rearrange("(p o) -> p o", o=1), writes=["subg"])
        ts("dve", subg, subg, lamc[:, 1:2], None, ALU.mult, None, ["subg", "lamc"], ["subg"])
        wb = [a_f32([128, 8, 512]) for _ in range(2)]
        for cb in range(18):
            r = cb % 2
            dma("sp", wb[r], I["w_ada"][l, :, cb * 512:(cb + 1) * 512].rearrange("(k p) c -> p k c", p=128),
                writes=["wb%d" % r])
            for m in range(4):
                q = cb * 4 + m
                ps, pk = ps_next()
                for k in range(8):
                    mm(ps[:, 0:5], wb[r][:, k, m * 128:(m + 1) * 128], silc[:, k, :], k == 0, k == 7,
                       ["wb%d" % r, "silc"], [pk])
                ts("dve", modT[:, q, :], ps[:, 0:5], vA[:, q:q + 1], None, ALU.add, None, [pk, "vA"], ["modT"])
        for i in range(3):
            c0 = (i * 3 + 1) * 8
            ts("dve", modT[:, c0:c0 + 8, :], modT[:, c0:c0 + 8, :], 1.0, None, ALU.add, None, ["modT"], ["modT"])
        for i in (0, 2):
            c0 = (i * 3 + 2) * 8
            ts("dve", modT[:, c0:c0 + 8, :], modT[:, c0:c0 + 8, :], 0.5, None, ALU.mult, None, ["modT"], ["modT"])

    def load_cast(dst, src, ncols, stg):
        A = src.shape[0] // 128
        n = 0
        for a in range(A):
            for c0 in range(0, ncols, 1024):
                cw = min(1024, ncols - c0)
                r = n % 2
                n += 1
                dma("sp", stg[r][:, 0:cw], src[a * 128:(a + 1) * 128, c0:c0 + cw], writes=["stg%d" % r])
                cp(cast_eng(), dst[:, a, c0:c0 + cw], stg[r][:, 0:cw], ["stg%d" % r], [("w", id(dst) % 9973, a)])

    def ln_stats(src, ksrc, N, hb, mean, rstd, t1):
        for k in range(8):
            cp(("pool", "dve")[k % 2], hb[:, k, 0:N], src[:, k, 0:N], [(ksrc, k)], [("h", k)])
            act(hb[:, 8 + k, 0:N], src[:, k, 0:N], AF.Square, [(ksrc, k)], [("h", 8 + k)])
        p1, k1 = ps_next()
        p2, k2 = ps_next()
        for k in range(8):
            mm(p1[:, 0:N], onesb, hb[:, k, 0:N], k == 0, k == 7, ["onesb", ("h", k)], [k1])
        for k in range(8):
            mm(p2[:, 0:N], onesb, hb[:, 8 + k, 0:N], k == 0, k == 7, ["onesb", ("h", 8 + k)], [k2])
        act(mean[:, 0:N], p1[:, 0:N], AF.Identity, [k1], ["mean"], scale=1.0 / D)
        tt("dve", t1[:, 0:N], mean[:, 0:N], mean[:, 0:N], ALU.mult, ["mean"], ["t1"])
        stt("dve", rstd[:, 0:N], p2[:, 0:N], 1.0 / D, t1[:, 0:N], ALU.mult, ALU.subtract, [k2, "t1"], ["rstd"])
        act(rstd[:, 0:N], rstd[:, 0:N], AF.Sqrt, ["rstd", "epsc"], ["rstd"], bias=epsc[:, 0:1])
        op("dve", lambda e: e.reciprocal(out=rstd[:, 0:N], in_=rstd[:, 0:N]), ["rstd"], ["rstd"])
        stt("dve", mean[:, 0:N], mean[:, 0:N], -1.0, rstd[:, 0:N], ALU.mult, ALU.mult, ["mean", "rstd"], ["mean"])

    def ln_apply(src, ksrc, N, mean, rstd, tmp, ktmp, outs):
        for k in range(8):
            e1 = ("dve", "pool")[k % 2]
            tt(e1, tmp[:, k, 0:N], src[:, k, 0:N], rstd[:, 0:N], ALU.mult, [(ksrc, k), "rstd"], [(ktmp, k)])
            tt(e1, tmp[:, k, 0:N], tmp[:, k, 0:N], mean[:, 0:N], ALU.add, [(ktmp, k), "mean"], [(ktmp, k)])
            for dst, kdst, segs in outs:
                for (c0, n, sc, bi) in segs:
                    act(dst[:, k, c0:c0 + n], tmp[:, k, c0:c0 + n], AF.Identity, [(ktmp, k), "modT", "vA"],
                        [(kdst, k)], bias=bi(k), scale=sc(k))

    def mod_ap(i, j, seq):
        return lambda k: modT[:, (i * 3 + j) * 8 + k, seq:seq + 1]

    def adaln_segs(i, segs):
        return [(c0, n, mod_ap(i, 1, sq), mod_ap(i, 0, sq)) for (c0, n, sq) in segs]

    def lnaff_segs(i, N):
        return [(0, N, lambda k: vA[:, 72 + i * 8 + k:73 + i * 8 + k], lambda k: vA[:, 96 + i * 8 + k:97 + i * 8 + k])]

    def ffn_phase(l, f):
        isub = 0 if f == 0 else 2
        a_reset(KEEP)
        wgu = a_bf16([128, 8, 2 * D_FF]); wdn = a_bf16([128, NJ, D])
        stg = [a_f32([128, 1024]) for _ in range(2)]
        x = a_f32([128, 8, TT]); z = a_f32([128, 8, TT]); u = a_bf16([128, 8, TT]); hb = a_bf16([128, NJ, TT])
        mean = a_f32([128, TT]); rstd = a_f32([128, TT]); t1 = a_f32([128, TT])
        sil = [a_f32([128, TT]) for _ in range(2)]
        load_cast(wgu, I["w_gu"][l, f], 2 * D_FF, stg)
        load_cast(wdn, I["w_down"][l, f], D, stg)
        kwgu = [("w", id(wgu) % 9973, a) for a in range(8)]
        kwdn = [("w", id(wdn) % 9973, a) for a in range(NJ)]
        for ti in range(NT):
            N, segs = tile_info(ti)
            dma("pool", x[:, :, 0:N], xT_s[ti, :, :, 0:N], reads=[("xT", ti)], writes=[("x", k) for k in range(8)])
            ln_stats(x, "x", N, hb, mean, rstd, t1)
            ln_apply(x, "x", N, mean, rstd, z, "z", [(u, "u", adaln_segs(isub, segs))])
            for j in range(NJ):
                pa, ka = ps_next()
                pb, kb = ps_next()
                for k in range(8):
                    mm(pa[:, 0:N], wgu[:, k, j * 128:(j + 1) * 128], u[:, k, 0:N], k == 0, k == 7,
                       [kwgu[k], ("u", k)], [ka])
                for k in range(8):
                    mm(pb[:, 0:N], wgu[:, k, D_FF + j * 128:D_FF + (j + 1) * 128], u[:, k, 0:N], k == 0, k == 7,
                       [kwgu[k], ("u", k)], [kb])
                r = j % 2
                act(sil[r][:, 0:N], pa[:, 0:N], AF.Silu, [ka], ["sil%d" % r])
                tt("dve", hb[:, j, 0:N], sil[r][:, 0:N], pb[:, 0:N], ALU.mult, ["sil%d" % r, kb], [("h", j)])
            for m in range(8):
                pf, kf = ps_next()
                for j in range(NJ):
                    mm(pf[:, 0:N], wdn[:, j, m * 128:(m + 1) * 128], hb[:, j, 0:N], j == 0, j == NJ - 1,
                       [kwdn[j], ("h", j)], [kf])
                for (c0, n, sq) in segs:
                    g = modT[:, (isub * 3 + 2) * 8 + m, sq:sq + 1]
                    ts("dve", z[:, m, c0:c0 + n], pf[:, c0:c0 + n], g, None, ALU.mult, None, [kf, "modT"], [("z", m)])
                stt("pool", z[:, m, 0:N], x[:, m, 0:N], ALPHA, z[:, m, 0:N], ALU.mult, ALU.add,
                    [("x", m), ("z", m)], [("z", m)])
            ln_stats(z, "z", N, hb, mean, rstd, t1)
            outs = [(x, "x", lnaff_segs(isub, N))]
            ln_apply(z, "z", N, mean, rstd, z, "z", outs)
            dma("pool", xT_s[ti, :, :, 0:N], x[:, :, 0:N], reads=[("x", k) for k in range(8)], writes=[("xT", ti)])
            if f == 0:
                ln_stats(x, "x", N, hb, mean, rstd, t1)
                ln_apply(x, "x", N, mean, rstd, z, "z", [(u, "u", adaln_segs(1, segs))])
                dma("pool", uT_s[ti, :, :, 0:N], u[:, :, 0:N], reads=[("u", k) for k in range(8)], writes=[("uT", ti)])

    def proj_phase(l):
        a_reset(KEEP)
        win = a_bf16([128, 8, N_IN])
        stg = [a_f32([128, 1024]) for _ in range(2)]
        load_cast(win, I["w_in"][l], N_IN, stg)
        kw = [("w", id(win) % 9973, a) for a in range(8)]
        u = a_bf16([128, 8, TT]); QT = a_bf16([128, 4, TT]); KT = a_bf16([128, 4, TT])
        prT = a_f32([128, 14, TT])
        kv32 = [a_f32([128, 512]) for _ in range(2)]
        vb = [a_bf16([128, 4, 128]) for _ in range(2)]
        nst = 0
        for ti in range(NT):
            N, segs = tile_info(ti)
            dma("sp", u[:, :, 0:N], uT_s[ti, :, :, 0:N], reads=[("uT", ti)], writes=["u"])
            for c in range(4):
                ps, pk = ps_next()
                for k in range(8):
                    mm(ps[:, 0:N], win[:, k, c * 128:(c + 1) * 128], u[:, k, 0:N], k == 0, k == 7, [kw[k], "u"], [pk])
                cp(ev_eng(), QT[:, c, 0:N], ps[:, 0:N], [pk], ["QT"])
            dma("pool", qT_s[ti, :, :, 0:N], QT[:, :, 0:N], reads=["QT"], writes=[("qT", ti)])
            for c in range(4):
                ps, pk = ps_next()
                for k in range(8):
                    mm(ps[:, 0:N], win[:, k, 512 + c * 128:512 + (c + 1) * 128], u[:, k, 0:N], k == 0, k == 7,
                       [kw[k], "u"], [pk])
                cp(ev_eng(), KT[:, c, 0:N], ps[:, 0:N], [pk], ["KT"])
            if ti < NTP:
                dma("pool", kT_s[:, :, ti * TT:ti * TT + N].rearrange("h p t -> p h t"), KT[:, :, 0:N],
                    reads=["KT"], writes=[("kT", ti)])
            else:
                for s_ in range(NSMP):
                    dma("pool", kTs_s[s_].rearrange("h p t -> p h t"), KT[:, :, s_ * 32:(s_ + 1) * 32],
                        reads=["KT"], writes=["kTs"])
            for c in range(14):
                ps, pk = ps_next()
                for k in range(8):
                    mm(ps[:, 0:N], win[:, k, N_ATT + c * 128:N_ATT + (c + 1) * 128], u[:, k, 0:N], k == 0, k == 7,
                       [kw[k], "u"], [pk])
                cp(ev_eng(), prT[:, c, 0:N], ps[:, 0:N], [pk], ["prT"])
            dma("pool", prw_s[ti, :, :, 0:N], prT[:, :, 0:N], reads=["prT"], writes=[("prw", ti)])
            for sb in range(N // 128):
                for which in range(2):
                    r = nst % 2
                    nst += 1
                    ps, pk = ps_next()
                    c0 = 512 + which * 512
                    for k in range(8):
                        mm(ps[:, 0:512], u[:, k, sb * 128:(sb + 1) * 128], win[:, k, c0:c0 + 512], k == 0, k == 7,
                           [kw[k], "u"], [pk])
                    cp(ev_eng(), kv32[r], ps[:, 0:512], [pk], ["kv32_%d" % r])
                    oname = ("nk_p", "nv_p")[which] if ti < NTP else ("nk_s", "nv_s")[which]
                    if ti < NTP:
                        dst = O[oname][l, ti * TT + sb * 128:ti * TT + (sb + 1) * 128, :]
                    else:
                        dst = O[oname][l]
                    dma("pool", dst, kv32[r], reads=["kv32_%d" % r])
                    if which == 1:
                        cp("pool", vb[r].rearrange("p h c -> p (h c)"), kv32[r], ["kv32_%d" % r], ["vb%d" % r])
                        if ti < NTP:
                            dma("sp", v_s[:, :, ti * (TT // 128) + sb, :].rearrange("h p c -> p h c"), vb[r],
                                reads=["vb%d" % r], writes=[("vs", ti)])
                        else:
                            for s_ in range(NSMP):
                                dma("sp", vs_s[s_].rearrange("h t c -> t h c"), vb[r][s_ * 32:(s_ + 1) * 32],
                                    reads=["vb%d" % r], writes=["vss"])

    def attn_phase(l):
        a_reset(KEEP)
        QT = a_bf16([128, 4, TT]); yaT = a_bf16([128, 4, TT])
        KB = 16
        KTb = [a_bf16([128, KB * 128]) for _ in range(2)]
        Vb = [a_bf16([128, KB, 128]) for _ in range(2)]
        P = [a_bf16([128, TT]) for _ in range(4)]
        rz = [a_f32([128, TT]) for _ in range(2)]
        tq = [a_f32([128, TT]) for _ in range(2)]
        oh = a_f32([128, TT]); sq = a_bf16([128, TT]); rs = a_f32([128, TT])
        ckin = [a_f32([128, 512]) for _ in range(2)]
        KTp = a_bf16([128, 4, max(PAST, 128)]); Vp = a_bf16([128, max(NPT, 1), 512])
        KTn = a_bf16([128, 4, DEC_SEQ]); Vn = a_bf16([DEC_SEQ, 4, 128])
        Obk = [(PS[0], "ps0"), (PS[1], "ps1")]
        Zbk = [(PS[2], "ps2"), (PS[3], "ps3")]
        cnt = {"s": 0, "p": 0}

        def attend(h, qrhs, nq, keytiles, ydst):
            nkt = len(keytiles)
            for idx, kt in enumerate(keytiles):
                q0, nk = kt["q0"], kt["nk"]
                for m in range(2):
                    sb_i = 4 + cnt["s"] % 4
                    cnt["s"] += 1
                    Sps, Sk = PS[sb_i], "ps%d" % sb_i
                    mm(Sps[0:nk, q0:nq], kt["kT"](m), qrhs(m, q0), True, True, kt["rk"] + ["QT"], [Sk])
                    pi = cnt["p"] % 4
                    cnt["p"] += 1
                    Pk = "P%d" % pi
                    act(P[pi][0:nk, q0:nq], Sps[0:nk, q0:nq], AF.Exp, [Sk], [Pk], scale=0.125)
                    for (c0, n, ebap) in kt["ebs"]:
                        tt("pool", P[pi][0:nk, c0:c0 + n], P[pi][0:nk, c0:c0 + n], ebap, ALU.mult, [Pk, "EB"], [Pk])
                    mm(Obk[m][0][:, q0:nq], kt["v"], P[pi][0:nk, q0:nq], idx == 0, idx == nkt - 1,
                       kt["rk"] + [Pk], [Obk[m][1]])
                    mm(Zbk[m][0][:, q0:nq], onesb[0:nk, :], P[pi][0:nk, q0:nq], idx == 0, idx == nkt - 1,
                       ["onesb", Pk], [Zbk[m][1]])
            op("dve", lambda e: e.reciprocal(out=rz[0][:, 0:nq], in_=Zbk[0][0][:, 0:nq]), [Zbk[0][1]], ["rz0"])
            op("dve", lambda e: e.reciprocal(out=rz[1][:, 0:nq], in_=Zbk[1][0][:, 0:nq]), [Zbk[1][1]], ["rz1"])
            ts("pool", rz[1][:, 0:nq], rz[1][:, 0:nq], lamc[:, 0:1], None, ALU.mult, None, ["rz1", "lamc"], ["rz1"])
            tt("dve", tq[0][:, 0:nq], Obk[0][0][:, 0:nq], rz[0][:, 0:nq], ALU.mult, [Obk[0][1], "rz0"], ["tq0"])
            tt("dve", tq[1][:, 0:nq], Obk[1][0][:, 0:nq], rz[1][:, 0:nq], ALU.mult, [Obk[1][1], "rz1"], ["tq1"])
            tt("pool", oh[:, 0:nq], tq[0][:, 0:nq], tq[1][:, 0:nq], ALU.add, ["tq0", "tq1"], ["oh"])
            act(sq[:, 0:nq], oh[:, 0:nq], AF.Square, ["oh"], ["sq"])
            sb_i = 4 + cnt["s"] % 4
            cnt["s"] += 1
            mm(PS[sb_i][:, 0:nq], onesb, sq[:, 0:nq], True, True, ["onesb", "sq"], ["ps%d" % sb_i])
            act(rs[:, 0:nq], PS[sb_i][:, 0:nq], AF.Sqrt, ["ps%d" % sb_i, "epsc"], ["rs"], bias=epsc[:, 0:1],
                scale=1.0 / 128)
            op("dve", lambda e: e.reciprocal(out=rs[:, 0:nq], in_=rs[:, 0:nq]), ["rs"], ["rs"])
            stt("dve", ydst, oh[:, 0:nq], subg[:, 0:1], rs[:, 0:nq], ALU.mult, ALU.mult, ["oh", "subg", "rs"], ["yaT"])

        nblk = 0
        for ti in range(NTP):
            dma("sp", QT, qT_s[ti], reads=[("qT", ti)], writes=["QT"])
            nkt = (TT // 128) * (ti + 1)
            for h in range(4):
                tiles = []
                for b0 in range(0, nkt, KB):
                    nb = min(KB, nkt - b0)
                    r = nblk % 2
                    nblk += 1
                    rk = [("kT", t_) for t_ in range(b0 * 128 // TT, ((b0 + nb) * 128 - 1) // TT + 1)]
                    rv = [("vs", t_) for t_ in range(b0 * 128 // TT, ((b0 + nb) * 128 - 1) // TT + 1)]
                    dma("sp", KTb[r][:, 0:nb * 128], kT_s[h, :, b0 * 128:(b0 + nb) * 128], reads=rk,
                        writes=["KTb%d" % r])
                    dma("pool", Vb[r][:, 0:nb, :], v_s[h, :, b0:b0 + nb, :], reads=rv, writes=["Vb%d" % r])
                    for j in range(nb):
                        kt = b0 + j
                        i = kt - (TT // 128) * ti
                        ebs = []
                        if i >= 0:
                            ebs.append((i * 128, 128, EB[:, h * 2 + 0, :]))
                        if 0 <= i + 1 < TT // 128:
                            ebs.append(((i + 1) * 128, 128, EB[:, h * 2 + 1, :]))
                        tiles.append(dict(
                            kT=(lambda m, r=r, j=j: KTb[r][m * 64:(m + 1) * 64, j * 128:(j + 1) * 128]),
                            v=Vb[r][:, j, :], nk=128, q0=max(0, i) * 128, ebs=ebs,
                            rk=["KTb%d" % r, "Vb%d" % r]))
                attend(h, lambda m, q0, h=h: QT[m * 64:(m + 1) * 64, h, q0:TT], TT, tiles, yaT[:, h, :])
            dma("pool", yaT_s[ti], yaT, reads=["yaT"], writes=[("yaT", ti)])
        ti = NTP
        dma("sp", QT[:, :, 0:128], qT_s[ti, :, :, 0:128], reads=[("qT", ti)], writes=["QT"])
        nci = 0
        for s_ in range(NSMP):
            for kt in range(NPT):
                for which in range(2):
                    r = nci % 2
                    nci += 1
                    src = (I["ck"], I["cv"])[which][l, s_, kt * 128:(kt + 1) * 128, :]
                    dma("sp", ckin[r], src, writes=["ckin%d" % r])
                    if which == 0:
                        for h in range(4):
                            ps, pk = ps_next()
                            tpose(ps[:, 0:128], ckin[r][:, h * 128:(h + 1) * 128], ident, ["ckin%d" % r, "ident"], [pk])
                            cp(ev_eng(), KTp[:, h, kt * 128:(kt + 1) * 128], ps[:, 0:128], [pk], ["KTp"])
                    else:
                        cp("pool", Vp[:, kt, :], ckin[r], ["ckin%d" % r], ["Vp"])
            dma("sp", KTn, kTs_s[s_].rearrange("h p t -> p h t"), reads=["kTs"], writes=["KTn"])
            dma("sp", Vn, vs_s[s_].rearrange("h t c -> t h c"), reads=["vss"], writes=["Vn"])
            for h in range(4):
                tiles = []
                for kt in range(NPT):
                    ebs = [(0, DEC_SEQ, EB[:, h * 2 + 1, 0:DEC_SEQ])] if kt == NPT - 1 else []
                    tiles.append(dict(kT=(lambda m, kt=kt, h=h: KTp[m * 64:(m + 1) * 64, h, kt * 128:(kt + 1) * 128]),
                                      v=Vp[:, kt, h * 128:(h + 1) * 128], nk=128, q0=0, ebs=ebs, rk=["KTp", "Vp"]))
                tiles.append(dict(kT=(lambda m, h=h: KTn[m * 64:(m + 1) * 64, h, :]), v=Vn[:, h, :], nk=DEC_SEQ, q0=0,
                                  ebs=[(0, DEC_SEQ, EB[0:DEC_SEQ, h * 2 + 0, 0:DEC_SEQ])], rk=["KTn", "Vn"]))
                attend(h, lambda m, q0, h=h, s_=s_: QT[m * 64:(m + 1) * 64, h, s_ * 32:(s_ + 1) * 32], DEC_SEQ, tiles,
                       yaT[:, h, s_ * 32:(s_ + 1) * 32])
        dma("pool", yaT_s[ti, :, :, 0:128], yaT[:, :, 0:128], reads=["yaT"], writes=[("yaT", ti)])

    def rwkv_phase(l):
        a_reset(KEEP)
        w2t = a_f32([64, 512]); a2t = a_f32([128, 512]); g2t = a_f32([128, 512])
        dma("sp", w2t, I["rw_w2"][l], writes=["w2t"])
        dma("sp", a2t[64:128, :], I["rw_a2"][l], writes=["a2t"])
        dma("sp", g2t, I["rw_g2"][l], writes=["g2t"])
        pr = a_f32([128, 14, TT]); xm = a_f32([128, 14, TT])
        xm4 = xm.rearrange("p c (s t) -> p c s t", t=64)
        tdw = a_f32([64, TT]); sgd = a_f32([128, TT])
        lw = [a_f32([128, 16, 64]) for _ in range(2)]
        aa = a_f32([128, 4, TT]); gT = a_f32([128, 4, TT]); kk = a_f32([128, 4, TT]); bon = a_f32([128, 4, TT])
        tmp = a_f32([128, 4, TT]); ee = a_f32([128, 16, 64])
        KR = a_f32([128, 4, 4, 128]); BK = a_f32([128, 4, 4, 128]); gC = a_f32([128, 4, 4])
        Vtm = a_f32([128, 4, 256]); Xb = a_f32([128, 4, 256]); Xk = a_f32([128, 4, 256])
        G1s = a_f32([128, 4, 128]); G2s = a_f32([128, 4, 128])
        Nn = [a_f32([128, 4, 64]) for _ in range(2)]
        NTs = [a_f32([128, 4, 64]) for _ in range(6)]
        Yb = [a_f32([128, 4, 64]) for _ in range(2)]
        Otm = a_f32([128, 4, 64]); Osq = a_f32([128, 4, 64]); Onr = a_f32([128, 4, 64]); gst = a_f32([128, 4, 4])
        prr = {"n": 0}

        def ps_pair():
            r = prr["n"] % 4
            prr["n"] += 1
            return (PS[2 * r], PS[2 * r + 1]), ["ps%d" % (2 * r), "ps%d" % (2 * r + 1)]

        HP = (slice(0, 64), slice(64, 128))
        Ht = a_f32([128, 4, 64])
        ybp = a_f32([128, 4, TT]); ybT = a_bf16([128, 4, TT])
        Sio = a_f32([64, 8, 64])
        mu = lambda c: vB[:, c:c + 1]
        memset("pool", Hst[:, 0], 0.0, [("H", 0)])
        memset("pool", shT[:, :, 0:1], 0.0, ["shT"])
        for s_ in range(NSMP):
            dma("sp", Sio, I["swkv"][l, s_].rearrange("h v k -> v h k"), writes=["Sio"])
            for cc in range(4):
                ps, pk = ps_next()
                tpose(ps[:, 0:64], Sio[:, 2 * cc:2 * cc + 2, :].rearrange("p h k -> p (h k)"), ident[0:64, 0:64],
                      ["Sio", "ident"], [pk])
                cp(ev_eng(), Hst[:, 1 + s_, cc, :], ps[:, 0:64], [pk], [("H", 1 + s_)])
            dma("sp", Sio[0:14, 0, 0:64], I["sshift"][l, s_].rearrange("(c p) -> c p", p=128)[:, 0:64], writes=["Sio"])
            dma("sp", Sio[0:14, 1, 0:64], I["sshift"][l, s_].rearrange("(c p) -> c p", p=128)[:, 64:128], writes=["Sio"])
            ps, pk = ps_next()
            tpose(ps[:, 0:14], Sio[0:14, 0:2, :].rearrange("p a b -> p (a b)"), ident[0:14, 0:14], ["Sio", "ident"], [pk])
            cp(ev_eng(), shT[:, :, 1 + s_], ps[:, 0:14], [pk], ["shT"])

        RW_CUT = 9
        for ti in range(NT if RW_CUT >= 1 else 0):
            N, segs = tile_info(ti)
            smp = ti >= NTP
            CV = 32 if smp else 64
            dma("sp", pr[:, :, 0:N], prw_s[ti, :, :, 0:N], reads=[("prw", ti)], writes=["pr"])
            if smp:
                memset("pool", xm, 0.0, ["xm"])
                pr4 = pr[:, :, 0:128].rearrange("p c (s t) -> p c s t", t=32)
                for s_ in range(NSMP):
                    tt("pool", xm4[:, :, s_, 1:32], pr4[:, :, s_, 0:31], pr4[:, :, s_, 1:32], ALU.subtract,
                       ["pr", "xm"], ["xm"])
                    tt("pool", xm4[:, :, s_, 0:1], shT[:, :, 1 + s_:2 + s_], pr4[:, :, s_, 0:1], ALU.subtract,
                       ["pr", "shT", "xm"], ["xm"])
                for c in range(14):
                    stt("dve", xm4[:, c, :, 0:32], xm4[:, c, :, 0:32], mu(c), pr4[:, c, :, :], ALU.mult, ALU.add,
                        ["xm", "pr", "vB"], ["xm"])
                for s_ in range(NSMP):
                    cp("pool", shT[:, :, 1 + s_:2 + s_], pr4[:, :, s_, 31:32], ["pr", "xm"], ["shT"])
            else:
                tt("pool", xm[:, :, 1:N], pr[:, :, 0:N - 1], pr[:, :, 1:N], ALU.subtract, ["pr"], ["xm"])
                tt("pool", xm[:, :, 0:1], shT[:, :, 0:1], pr[:, :, 0:1], ALU.subtract, ["pr", "shT", "xm"], ["xm"])
                for c in range(14):
                    stt("dve", xm[:, c, :], xm[:, c, :], mu(c), pr[:, c, :], ALU.mult, ALU.add,
                        ["xm", "pr", "vB"], ["xm"])
                cp("pool", shT[:, :, 0:1], pr[:, :, N - 1:N], ["pr", "xm"], ["shT"])
            act(tdw, xm[0:64, 12, :], AF.Tanh, ["xm"], ["tdw"])
            act(sgd, xm[:, 13, :], AF.Sigmoid, ["xm"], ["sgd"])
            lw0 = lw[0].rearrange("p (c s) t -> p c (s t)", c=4)
            for cc in range(4):
                ps, pk = ps_next()
                mm(ps[:, 0:TT], w2t[:, cc * 128:(cc + 1) * 128], tdw, True, True, ["w2t", "tdw"], [pk])
                act(lw0[:, cc, :], ps[:, 0:TT], AF.Sigmoid, [pk, "vB"], ["lw0"], bias=vB[:, 14 + cc:15 + cc])
                ps, pk = ps_next()
                mm(ps[:, 0:TT], a2t[64:128, cc * 128:(cc + 1) * 128], xm[64:128, 12, :], True, True, ["a2t", "xm"], [pk])
                act(aa[:, cc, :], ps[:, 0:TT], AF.Sigmoid, [pk, "vB"], ["aa"], bias=vB[:, 18 + cc:19 + cc])
                ps, pk = ps_next()
                mm(ps[:, 0:TT], g2t[:, cc * 128:(cc + 1) * 128], sgd, True, True, ["g2t", "sgd"], [pk])
                cp(ev_eng(), gT[:, cc, :], ps[:, 0:TT], [pk], ["gT"])
            ts("pool", lw[0], lw[0], -0.6065306597126334, None, ALU.mult, None, ["lw0"], ["lw0"])
            for cc in range(4):
                ts("pool", kk[:, cc, :], xm[:, 4 + cc, :], vB[:, 22 + cc:23 + cc], None, ALU.mult, None,
                   ["xm", "vB"], ["kk"])
                tt("pool", tmp[:, cc, :], kk[:, cc, :], kk[:, cc, :], ALU.mult, ["kk"], ["tmp"])
                ps, pk = ps_next()
                mm(ps[:, 0:TT], hsum, tmp[:, cc, :], True, True, ["hsum", "tmp"], [pk])
                ts("dve", tmp[:, cc, :], ps[:, 0:TT], 1e-24, None, ALU.max, None, [pk], ["tmp"])
                act(tmp[:, cc, :], tmp[:, cc, :], AF.Sqrt, ["tmp"], ["tmp"])
                op("dve", lambda e, cc=cc: e.reciprocal(out=tmp[:, cc, :], in_=tmp[:, cc, :]), ["tmp"], ["tmp"])
                tt("pool", kk[:, cc, :], kk[:, cc, :], tmp[:, cc, :], ALU.mult, ["kk", "tmp"], ["kk"])
                ts("dve", tmp[:, cc, :], aa[:, cc, :], -1.0, vB[:, 26 + cc:27 + cc], ALU.add, ALU.mult,
                   ["aa", "vB", "tmp"], ["tmp"])
                stt("dve", xm[:, 4 + cc, :], tmp[:, cc, :], 1.0, xm[:, 4 + cc, :], ALU.add, ALU.mult,
                    ["tmp", "xm"], ["xm"])
                tt("pool", aa[:, cc, :], aa[:, cc, :], kk[:, cc, :], ALU.mult, ["aa", "kk"], ["aa"])
                stt("dve", tmp[:, cc, :], xm[:, cc, :], vB[:, 54 + cc:55 + cc], xm[:, 4 + cc, :], ALU.mult, ALU.mult,
                    ["xm", "vB", "tmp"], ["tmp"])
                ps, pk = ps_next()
                mm(ps[:, 0:TT], hsum, tmp[:, cc, :], True, True, ["hsum", "tmp"], [pk])
                tt("dve", bon[:, cc, :], ps[:, 0:TT], xm[:, 8 + cc, :], ALU.mult, [pk, "xm"], ["bon"])
            cur = 0
            for sh_ in (1, 2, 4, 8, 16, 32):
                a_, b_ = lw[cur], lw[1 - cur]
                tt("pool", b_[:, :, sh_:64], a_[:, :, sh_:64], a_[:, :, 0:64 - sh_], ALU.add,
                   ["lw%d" % cur], ["lw%d" % (1 - cur)])
                cp("pool", b_[:, :, 0:sh_], a_[:, :, 0:sh_], ["lw%d" % cur], ["lw%d" % (1 - cur)])
                cur = 1 - cur
            Lc = lw[cur]
            kcur = "lw%d" % cur
            KR5 = KR.rearrange("p c s (w t) -> p (c s) w t", w=2)
            BK5 = BK.rearrange("p c s (w t) -> p (c s) w t", w=2)
            v16 = lambda ap: ap.rearrange("p c (s t) -> p (c s) t", t=64)
            act(ee, Lc, AF.Exp, [kcur], ["ee"])
            tt("dve", KR5[:, :, 1, :], v16(xm[:, 0:4, :]), ee, ALU.mult, ["xm", "ee"], ["KR"])
            cp("pool", gC.rearrange("p c s -> p (c s)"), ee[:, :, CV - 1], ["ee"], ["gC"])
            act(ee, Lc, AF.Exp, [kcur, "gC", "KR"], ["ee"], scale=-1.0)
            tt("dve", BK5[:, :, 0, :], v16(aa), ee, ALU.mult, ["aa", "ee"], ["BK"])
            tt("pool", BK5[:, :, 1, :], v16(xm[:, 4:8, :]), ee, ALU.mult, ["xm", "ee"], ["BK"])
            memset("pool", tmp, 0.0, ["tmp"])
            cp("pool", v16(tmp)[:, :, 1:64], Lc[:, :, 0:63], [kcur, "tmp"], ["tmp"])
            act(ee, v16(tmp), AF.Exp, ["tmp", "BK"], ["ee"])
            tt("dve", KR5[:, :, 0, :], v16(kk), ee, ALU.mult, ["kk", "ee"], ["KR"])
            xv4 = xm[:, 8:12, :].rearrange("p c (s t) -> p c s t", t=64)
            for ch in range(4):
                for (dstb, src, ksrc, kd) in ((Vtm, None, "xm", "Vtm"), (Xb, 0, "BK", "Xb"), (Xk, 1, "BK", "Xk")):
                    pp, pkk = ps_pair()
                    for h2 in range(2):
                        P_ = HP[h2]
                        for cc in range(4):
                            in_ = xv4[P_, cc, ch, :] if src is None else BK[P_, cc, ch, src * 64:(src + 1) * 64]
                            mm(pp[h2][P_, cc * 64:(cc + 1) * 64], in_, ident[P_, P_], True, True, [ksrc, "ident"], [pkk[h2]])
                        cp(ev_eng(), dstb[P_, ch, :], pp[h2][P_, 0:256], [pkk[h2]], [(kd, ch)])
            for ch in range(4 if RW_CUT >= 2 else 0):
                seq = (1 + ch) if smp else 0
                H = Hst[:, seq]
                kH = ("H", seq)
                v3 = lambda ap, t: ap.rearrange("p (c t) -> p c t", t=t)
                for (lsrc, gdst, msk, kg, kmsk) in ((0, G1s, maskg, "G1s", "maskg"), (1, G2s, maskgk, "G2s", "maskgk")):
                    pp, pkk = ps_pair()
                    for h2 in range(2):
                        P_ = HP[h2]
                        for cc in range(4):
                            mm(pp[h2][P_, cc * 128:(cc + 1) * 128], BK[P_, cc, ch, lsrc * 64:(lsrc + 1) * 64],
                               KR[P_, cc, ch, :], True, True, ["BK", "KR"], [pkk[h2]])
                        tt("dve", gdst[P_], v3(pp[h2][P_, 0:512], 128), msk[P_], ALU.mult, [pkk[h2], kmsk], [kg])
                pp, pkk = ps_pair()
                for h2 in range(2):
                    P_ = HP[h2]
                    for cc in range(4):
                        mm(pp[h2][P_, cc * 64:(cc + 1) * 64], KR[P_, cc, ch, 0:64], BK[P_, cc, ch, 0:64],
                           True, True, ["BK", "KR"], [pkk[h2]])
                    tt("dve", Nn[0][P_], v3(pp[h2][P_, 0:256], 64), maska[P_], ALU.mult, [pkk[h2], "maska"], ["Nn0"])
                cp("pool", NTs[0], G1s[:, :, 0:64], ["G1s"], ["NT0"])
                if RW_CUT == 2:
                    continue
                for j in range(5):
                    a_, b_ = Nn[j % 2], Nn[(j + 1) % 2]
                    ka, kb = "Nn%d" % (j % 2), "Nn%d" % ((j + 1) % 2)
                    pp, pkk = ps_pair()
                    for h2 in range(2):
                        P_ = HP[h2]
                        for cc in range(4):
                            mm(pp[h2][P_, cc * 64:(cc + 1) * 64], a_[P_, cc, :], NTs[j][P_, cc, :], True, True,
                               [ka, "NT%d" % j], [pkk[h2]])
                        cp(ev_eng(), NTs[j + 1][P_], v3(pp[h2][P_, 0:256], 64), [pkk[h2]], ["NT%d" % (j + 1)])
                    if j < 4:
                        pp, pkk = ps_pair()
                        for h2 in range(2):
                            P_ = HP[h2]
                            for cc in range(4):
                                mm(pp[h2][P_, cc * 64:(cc + 1) * 64], NTs[j][P_, cc, :], a_[P_, cc, :], True, True,
                                   [ka, "NT%d" % j], [pkk[h2]])
                            cp(ev_eng(), b_[P_], v3(pp[h2][P_, 0:256], 64), [pkk[h2]], [kb])
                if RW_CUT == 3:
                    continue
                pp, pkk = ps_pair()
                for h2 in range(2):
                    P_ = HP[h2]
                    for cc in range(4):
                        o_ = pp[h2][P_, cc * 64:(cc + 1) * 64]
                        mm(o_, KR[P_, cc, ch, 0:64], H[P_, cc, :], True, False, ["KR", kH], [pkk[h2]])
                        mm(o_, G2s[P_, cc, 0:64], Vtm[P_, ch, cc * 64:(cc + 1) * 64], False, True,
                           ["G2s", ("Vtm", ch)], [pkk[h2]])
                    act(Yb[0][P_], v3(pp[h2][P_, 0:256], 64), AF.Identity, [pkk[h2]], ["Y0"], scale=-1.0)
                yc = 0
                for j in range(6):
                    pp, pkk = ps_pair()
                    for h2 in range(2):
                        P_ = HP[h2]
                        for cc in range(4):
                            mm(pp[h2][P_, cc * 64:(cc + 1) * 64], NTs[j][P_, cc, :], Yb[yc][P_, cc, :], True, True,
                               ["NT%d" % j, "Y%d" % yc], [pkk[h2]])
                        tt("dve", Yb[1 - yc][P_], Yb[yc][P_], v3(pp[h2][P_, 0:256], 64), ALU.add,
                           ["Y%d" % yc, pkk[h2]], ["Y%d" % (1 - yc)])
                    yc = 1 - yc
                U, kU = Yb[yc], "Y%d" % yc
                if RW_CUT == 4:
                    continue
                pp, pkk = ps_pair()
                for h2 in range(2):
                    P_ = HP[h2]
                    for cc in range(4):
                        o_ = pp[h2][P_, cc * 64:(cc + 1) * 64]
                        mm(o_, KR[P_, cc, ch, 64:128], H[P_, cc, :], True, False, ["KR", kH], [pkk[h2]])
                        mm(o_, G1s[P_, cc, 64:128], U[P_, cc, :], False, False, ["G1s", kU], [pkk[h2]])
                        mm(o_, G2s[P_, cc, 64:128], Vtm[P_, ch, cc * 64:(cc + 1) * 64], False, True,
                           ["G2s", ("Vtm", ch)], [pkk[h2]])
                    cp("act", Otm[P_], v3(pp[h2][P_, 0:256], 64), [pkk[h2]], ["Otm"])
                if RW_CUT == 5:
                    continue
                pp, pkk = ps_pair()
                for h2 in range(2):
                    P_ = HP[h2]
                    for cc in range(4):
                        o_ = pp[h2][P_, cc * 64:(cc + 1) * 64]
                        mm(o_, Xb[P_, ch, cc * 64:(cc + 1) * 64], U[P_, cc, :], True, False, [("Xb", ch), kU], [pkk[h2]])
                        mm(o_, Xk[P_, ch, cc * 64:(cc + 1) * 64], Vtm[P_, ch, cc * 64:(cc + 1) * 64], False, True,
                           [("Xk", ch), ("Vtm", ch)], [pkk[h2]])
                    tt("dve", Ht[P_], v3(pp[h2][P_, 0:256], 64), H[P_], ALU.add, [pkk[h2], kH], ["Ht"])
                for cc in range(4):
                    ts("pool", H[:, cc, :], Ht[:, cc, :], gC[:, cc, ch:ch + 1], None, ALU.mult, None, ["Ht", "gC"], [kH])
                if RW_CUT == 6:
                    continue
                op("dve", lambda e: e.reduce_sum(out=gst[:, 0, :], in_=Otm, axis=AX.X), ["Otm"], ["gst"])
                tt("pool", Osq, Otm, Otm, ALU.mult, ["Otm"], ["Osq"])
                op("dve", lambda e: e.reduce_sum(out=gst[:, 1, :], in_=Osq, axis=AX.X), ["Osq"], ["gst"])
                ts("dve", gst[:, 0, :], gst[:, 0, :], 1.0 / 64, None, ALU.mult, None, ["gst"], ["gst"])
                tt("dve", gst[:, 2, :], gst[:, 0, :], gst[:, 0, :], ALU.mult, ["gst"], ["gst"])
                stt("dve", gst[:, 1, :], gst[:, 1, :], 1.0 / 64, gst[:, 2, :], ALU.mult, ALU.subtract, ["gst"], ["gst"])
                act(gst[:, 1, :], gst[:, 1, :], AF.Sqrt, ["gst", "epsc"], ["gst"], bias=epsc[:, 1:2])
                op("dve", lambda e: e.reciprocal(out=gst[:, 1, :], in_=gst[:, 1, :]), ["gst"], ["gst"])
                for cc in range(4):
                    ts("dve", Onr[:, cc, :], Otm[:, cc, :], gst[:, 0, cc:cc + 1], gst[:, 1, cc:cc + 1],
                       ALU.subtract, ALU.mult, ["Otm", "gst"], ["Onr"])
                pp, pkk = ps_pair()
                for h2 in range(2):
                    P_ = HP[h2]
                    for cc in range(4):
                        mm(pp[h2][P_, cc * 64:(cc + 1) * 64], Onr[P_, cc, :], ident[P_, P_], True, True, ["Onr", "ident"], [pkk[h2]])
                    for cc in range(4):
                        act(ybp[P_, cc, ch * 64:(ch + 1) * 64], pp[h2][P_, cc * 64:(cc + 1) * 64], AF.Identity,
                            [pkk[h2], "vB"], ["ybp"], bias=vB[P_, 34 + cc:35 + cc], scale=vB[P_, 30 + cc:31 + cc])
            tt("pool", ybp, ybp, bon, ALU.add, ["ybp", "bon"], ["ybp"])
            if smp:
                yv = ybp.rearrange("p c (s t) -> p c s t", t=64)[:, :, :, 0:32]
                gv = gT.rearrange("p c (s t) -> p c s t", t=64)[:, :, :, 0:32]
                tt("dve", ybT[:, :, 0:128].rearrange("p c (s t) -> p c s t", t=32), yv, gv, ALU.mult,
                   ["ybp", "gT"], ["ybT"])
            else:
                tt("dve", ybT, ybp, gT, ALU.mult, ["ybp", "gT"], ["ybT"])
            dma("pool", ybT_s[ti, :, :, 0:N], ybT[:, :, 0:N], reads=["ybT"], writes=[("ybT", ti)])
        for seq in range(5):
            for cc in range(4):
                ps, pk = ps_next()
                tpose(ps[0:64, 0:128], Hst[:, seq, cc, :], ident, [("H", seq), "ident"], [pk])
                cp(ev_eng(), Sio[:, 2 * cc:2 * cc + 2, :].rearrange("p h k -> p (h k)"), ps[0:64, 0:128], [pk], ["Sio"])
            dst = O["nwkv_p"][l] if seq == 0 else O["nwkv_s"][l, seq - 1]
            dma("pool", dst.rearrange("h v k -> v h k"), Sio, reads=["Sio"], writes=["nwkv"])
            dsh = O["nsh_p"][l] if seq == 0 else O["nsh_s"][l, seq - 1]
            ps, pk = ps_next()
            tpose(ps[0:14, 0:128], shT[:, :, seq], ident, ["shT", "ident"], [pk])
            cp(ev_eng(), Sio[0:14, 0:2, :].rearrange("p a b -> p (a b)"), ps[0:14, 0:128], [pk], ["Sio"])
            dma("pool", dsh.rearrange("(c p) -> c p", p=128), Sio[0:14, 0:2, :].rearrange("p a b -> p (a b)"),
                reads=["Sio"], writes=["nsh"])

    def merge_phase(l):
        a_reset(KEEP)
        wg = a_bf16([128, 8, 2 * D]); wba = a_bf16([128, 4, D]); wbb = a_bf16([128, 4, D]); wo = a_bf16([128, 8, D])
        stg = [a_f32([128, 1024]) for _ in range(2)]
        load_cast(wg, I["w_gate"][l], 2 * D, stg)
        load_cast(wba, I["w_br_a"][l], D, stg)
        load_cast(wbb, I["w_br_b"][l], D, stg)
        load_cast(wo, I["w_o"][l], D, stg)
        kq = lambda w, n: [("w", id(w) % 9973, a) for a in range(n)]
        kwg, kwba, kwbb, kwo = kq(wg, 8), kq(wba, 4), kq(wbb, 4), kq(wo, 8)
        u = a_bf16([128, 8, TT]); ya = a_bf16([128, 4, TT]); yb = a_bf16([128, 4, TT])
        x = a_f32([128, 8, TT]); z = a_f32([128, 8, TT]); mg = a_bf16([128, 8, TT]); hb = a_bf16([128, 16, TT])
        mean = a_f32([128, TT]); rstd = a_f32([128, TT]); t1 = a_f32([128, TT])
        ga = [a_f32([128, TT]) for _ in range(2)]
        tq = [a_f32([128, TT]) for _ in range(2)]
        for ti in range(NT):
            N, segs = tile_info(ti)
            dma("sp", u[:, :, 0:N], uT_s[ti, :, :, 0:N], reads=[("uT", ti)], writes=["u"])
            dma("sp", ya[:, :, 0:N], yaT_s[ti, :, :, 0:N], reads=[("yaT", ti)], writes=["ya"])
            dma("pool", yb[:, :, 0:N], ybT_s[ti, :, :, 0:N], reads=[("ybT", ti)], writes=["yb"])
            dma("pool", x[:, :, 0:N], xT_s[ti, :, :, 0:N], reads=[("xT", ti)], writes=[("x", k) for k in range(8)])
            for m in range(8):
                pga, kga = ps_next()
                pgb, kgb = ps_next()
                pba, kba = ps_next()
                pbb, kbb = ps_next()
                for k in range(8):
                    mm(pga[:, 0:N], wg[:, k, m * 128:(m + 1) * 128], u[:, k, 0:N], k == 0, k == 7, [kwg[k], "u"], [kga])
                for k in range(8):
                    mm(pgb[:, 0:N], wg[:, k, D + m * 128:D + (m + 1) * 128], u[:, k, 0:N], k == 0, k == 7,
                       [kwg[k], "u"], [kgb])
                for k in range(4):
                    mm(pba[:, 0:N], wba[:, k, m * 128:(m + 1) * 128], ya[:, k, 0:N], k == 0, k == 3, [kwba[k], "ya"], [kba])
                for k in range(4):
                    mm(pbb[:, 0:N], wbb[:, k, m * 128:(m + 1) * 128], yb[:, k, 0:N], k == 0, k == 3, [kwbb[k], "yb"], [kbb])
                act(ga[0][:, 0:N], pga[:, 0:N], AF.Sigmoid, [kga, "vB"], ["ga0"], bias=vB[:, 38 + m:39 + m])
                act(ga[1][:, 0:N], pgb[:, 0:N], AF.Sigmoid, [kgb, "vB"], ["ga1"], bias=vB[:, 46 + m:47 + m])
                tt("dve", tq[0][:, 0:N], ga[0][:, 0:N], pba[:, 0:N], ALU.mult, ["ga0", kba], ["tq0"])
                tt("dve", tq[1][:, 0:N], ga[1][:, 0:N], pbb[:, 0:N], ALU.mult, ["ga1", kbb], ["tq1"])
                tt("pool", mg[:, m, 0:N], tq[0][:, 0:N], tq[1][:, 0:N], ALU.add, ["tq0", "tq1"], [("mg", m)])
            for m in range(8):
                po, ko = ps_next()
                for k in range(8):
                    mm(po[:, 0:N], wo[:, k, m * 128:(m + 1) * 128], mg[:, k, 0:N], k == 0, k == 7, [kwo[k], ("mg", k)], [ko])
                for (c0, n, sq) in segs:
                    g = modT[:, (1 * 3 + 2) * 8 + m, sq:sq + 1]
                    ts("dve", z[:, m, c0:c0 + n], po[:, c0:c0 + n], g, None, ALU.mult, None, [ko, "modT"], [("z", m)])
                stt("dve", z[:, m, 0:N], x[:, m, 0:N], ALPHA, z[:, m, 0:N], ALU.mult, ALU.add,
                    [("x", m), ("z", m)], [("z", m)])
            ln_stats(z, "z", N, hb, mean, rstd, t1)
            ln_apply(z, "z", N, mean, rstd, z, "z", [(x, "x", lnaff_segs(1, N))])
            dma("pool", xT_s[ti, :, :, 0:N], x[:, :, 0:N], reads=[("x", k) for k in range(8)], writes=[("xT", ti)])

    def final_phase():
        a_reset(KEEP)
        xi = [a_f32([128, 8, 128]) for _ in range(2)]
        xo = [a_f32([128, D]) for _ in range(2)]
        n = 0
        for ti in range(NT):
            N, _ = tile_info(ti)
            for sb in range(N // 128):
                r = n % 2
                n += 1
                dma("sp", xi[r], xT_s[ti, :, :, sb * 128:(sb + 1) * 128], reads=[("xT", ti)], writes=["xi%d" % r])
                for k in range(8):
                    ps, pk = ps_next()
                    tpose(ps[:, 0:128], xi[r][:, k, :], ident, ["xi%d" % r, "ident"], [pk])
                    cp(ev_eng(), xo[r][:, k * 128:(k + 1) * 128], ps[:, 0:128], [pk], [("xo", r, k)])
                dst = O["y_p"][ti * TT + sb * 128: ti * TT + (sb + 1) * 128, :] if ti < NTP else O["y_s"]
                dma("pool", dst, xo[r], reads=[("xo", r, k) for k in range(8)])

    return dict(nc=nc, S=S, I=I, O=O, phase=dict(layer_setup=layer_setup, ffn=ffn_phase, final=final_phase, proj=proj_phase, attn=attn_phase, rwkv=rwkv_phase, merge=merge_phase),
                barrier=barrier, env=locals())


def build_full(SEQ, L, PAST, stages=None, dbg=None, depth=None):
    B = build(SEQ, L, PAST, dbg, depth)
    ph = B["phase"]
    bar = B["barrier"]
    for l in range(L):
        ph["layer_setup"](l); bar()
        ph["ffn"](l, 0); bar()
        if stages is None or "proj" in stages:
            ph["proj"](l); bar()
        if stages is None or "attn" in stages:
            ph["attn"](l); bar()
        if stages is None or "rwkv" in stages:
            ph["rwkv"](l); bar()
        if stages is None or "merge" in stages:
            ph["merge"](l); bar()
        if stages is None or "ffn2" in stages:
            ph["ffn"](l, 1); bar()
    ph["final"]()
    B["S"].finish()
    B["S"].close()
    return B["nc"]


_CACHE = {}


def _host_inputs(inp, L):
    f = lambda a: np.ascontiguousarray(np.asarray(a, dtype=np.float32))
    vecA = np.concatenate([f(inp["b_ada"]).reshape(L, 72, 128), f(inp["ln_g"]).reshape(L, 24, 128),
                           f(inp["ln_b"]).reshape(L, 24, 128)], axis=1)
    vecB = np.concatenate([f(inp["rw_mu"]).reshape(L, 14, 128), f(inp["rw_w0"]).reshape(L, 4, 128),
                           f(inp["rw_a0"]).reshape(L, 4, 128), f(inp["rw_k_k"]).reshape(L, 4, 128),
                           f(inp["rw_k_a"]).reshape(L, 4, 128), f(inp["rw_lnx_g"]).reshape(L, 4, 128),
                           f(inp["rw_lnx_b"]).reshape(L, 4, 128), f(inp["b_gate"]).reshape(L, 16, 128),
                           f(inp["rw_r_k"]).reshape(L, 4, 128)], axis=1)
    lamv = np.array([[0.8 - 0.6 * math.exp(-0.3 * l), 1.0 - (0.8 - 0.6 * math.exp(-0.3 * l))] for l in range(L)],
                    np.float32)
    shared = {"rel_bias": f(inp["rel_bias"]).reshape(1, 128), "w_ada": f(inp["w_ada"]), "vecA": f(vecA),
              "vecB": f(vecB), "w_gu": f(inp["w_gu"]), "w_down": f(inp["w_down"]), "w_in": f(inp["w_in"]),
              "lam_qk": f(inp["lam_qk"]).reshape(L, 256), "subln_g": f(inp["subln_g"]), "lamv": lamv,
              "rw_w2": f(inp["rw_w2"]), "rw_a2": f(inp["rw_a2"]), "rw_g2": f(inp["rw_g2"]),
              "w_br_a": f(inp["w_br_a"]), "w_br_b": f(inp["w_br_b"]), "w_gate": f(inp["w_gate"]),
              "w_o": f(inp["w_o"])}
    return shared


_PER_LAYER = ("w_ada", "vecA", "vecB", "w_gu", "w_down", "w_in", "lam_qk", "subln_g", "lamv", "rw_w2", "rw_a2",
              "rw_g2", "w_br_a", "w_br_b", "w_gate", "w_o")


def kernel(**inp):
    f = lambda a: np.ascontiguousarray(np.asarray(a, dtype=np.float32))
    xp = f(inp["x_prompt"])
    BATCH, SEQ = xp.shape[0], xp.shape[1]
    L = np.asarray(inp["w_ada"]).shape[0]
    ck, cv = f(inp["cache_k"]), f(inp["cache_v"])
    PAST = ck.shape[2]
    key = (SEQ, L, PAST)
    if key not in _CACHE:
        _CACHE[key] = build_full(SEQ, 1, PAST, depth=L)
    nc = _CACHE[key]
    shared = _host_inputs(inp, L)
    consts = _consts()
    xs = f(inp["x_sample"])
    cp_, cs = f(inp["c_prompt"]), f(inp["c_sample"])
    swkv, ssh = f(inp["state_wkv"]), f(inp["state_shift"])
    cur_p = [xp[c % BATCH] for c in range(8)]
    cur_s = [f(xs[c * NSMP:(c + 1) * NSMP].reshape(128, D)) for c in range(8)]
    per_layer = []
    for l in range(L):
        maps = []
        for c in range(8):
            sl = slice(c * NSMP, (c + 1) * NSMP)
            m = {"rel_bias": shared["rel_bias"]}
            for k in _PER_LAYER:
                m[k] = f(shared[k][l:l + 1])
            m.update(consts)
            m["xp"] = f(cur_p[c]); m["xs"] = f(cur_s[c])
            m["cc"] = f(np.concatenate([cp_[c % BATCH:c % BATCH + 1], cs[sl]], axis=0))
            m["ck"] = f(ck[l:l + 1, sl].reshape(1, NSMP, PAST, 512))
            m["cv"] = f(cv[l:l + 1, sl].reshape(1, NSMP, PAST, 512))
            m["swkv"] = f(swkv[l:l + 1, sl]); m["sshift"] = f(ssh[l:l + 1, sl])
            maps.append(m)
        res = run_bass_kernel_spmd(nc, maps, core_ids=list(range(8))).results
        cur_p = [res[c]["y_p"] for c in range(8)]
        cur_s = [res[c]["y_s"] for c in range(8)]
        per_layer.append(res)
    res = per_layer[-1]
    y_p = np.stack([res[b]["y_p"] for b in range(BATCH)])
    y_s = np.concatenate([res[c]["y_s"].reshape(NSMP, DEC_SEQ, D) for c in range(8)])
    cat = lambda name, cores, ax: np.concatenate(
        [np.stack([per_layer[l][c][name][0] for c in cores], axis=0)[None] for l in range(L)], axis=0)
    nk_p = cat("nk_p", range(BATCH), 0).reshape(L, BATCH, SEQ, 4, 128)
    nv_p = cat("nv_p", range(BATCH), 0).reshape(L, BATCH, SEQ, 4, 128)
    nwkv_p = cat("nwkv_p", range(BATCH), 0)
    nsh_p = cat("nsh_p", range(BATCH), 0)
    nk_s = cat("nk_s", range(8), 0).reshape(L, 8 * NSMP, DEC_SEQ, 4, 128)
    nv_s = cat("nv_s", range(8), 0).reshape(L, 8 * NSMP, DEC_SEQ, 4, 128)
    nwkv_s = cat("nwkv_s", range(8), 0).reshape(L, 8 * NSMP, 8, 64, 64)
    nsh_s = cat("nsh_s", range(8), 0).reshape(L, 8 * NSMP, N_RWKV)
    outs = (y_p, y_s, nk_p, nv_p, nwkv_p, nsh_p, nk_s, nv_s, nwkv_s, nsh_s)
    return tuple(np.ascontiguousarray(o, dtype=np.float32) for o in outs)
```
